# Optimizing a Trainium2 kernel written in Bass

```python
import math
import numpy as np
import jax, jax.numpy as jnp
from jax import lax

D_MODEL = 4096
BATCH = 4
SEQ = 2048
DEPTH = 2

HEAD_DIM = 128
HG_HEADS = 8
HG_DK = 128
HG_DV = 128
LB_FLOOR = 1e-30
GD_HEADS = 8
GD_DK = 128
GD_DV = 128
CONV_W = 4
NSA_HEADS = 16
NSA_KV = 4
NSA_HPG = NSA_HEADS // NSA_KV
CMP_LEN = 32
CMP_STRIDE = 16
CMP_HIDDEN = 256
SLC_LEN = 64
SLC_TOP = 16
WINDOW = 512
WIN_BLOCK = 128
SLC_Q_CHUNK = 32
CHUNK = 64
ROPE_THETA = 10000.0
NORM_EPS = 1e-6
NEG = -1e30
FFN_HIDDEN = -(-8 * D_MODEL // (3 * 256)) * 256

HG_W = HG_HEADS * HG_DK
GD_W = GD_HEADS * GD_DK
NSA_W = NSA_HEADS * HEAD_DIM
NSA_KV_W = NSA_KV * HEAD_DIM
MIX_W = HG_W + GD_W + NSA_W
GD_CONV_C = GD_HEADS * (2 * GD_DK + GD_DV)
IN_SIZES = (HG_W, HG_W, HG_HEADS * HG_DV, HG_HEADS * HG_DV,
            GD_W, GD_W, GD_HEADS * GD_DV, GD_HEADS * GD_DV,
            GD_HEADS, GD_HEADS,
            NSA_W,
            NSA_KV_W, NSA_KV_W, NSA_KV_W, NSA_KV_W, NSA_KV_W, NSA_KV_W,
            NSA_HEADS * 3)
N_IN = sum(IN_SIZES)

kernel_name = "hymba_hgrn2_gdn_nsa_hybrid"


def _rmsnorm(x, w):
    xf = x.astype(jnp.float32)
    y = xf * lax.rsqrt(jnp.mean(xf * xf, axis=-1, keepdims=True) + NORM_EPS)
    return (y * w.astype(jnp.float32)).astype(x.dtype)


def _rope_tables(T):
    inv = ROPE_THETA ** (-jnp.arange(0, HEAD_DIM, 2, dtype=jnp.float32) / HEAD_DIM)
    ang = jnp.arange(T, dtype=jnp.float32)[:, None] * inv[None, :]
    return jnp.cos(ang), jnp.sin(ang)


def _rope(x, cos, sin):
    half = HEAD_DIM // 2
    xf = x.astype(jnp.float32)
    x1, x2 = xf[..., :half], xf[..., half:]
    c, s = cos[:, None, :], sin[:, None, :]
    return jnp.concatenate([x1 * c - x2 * s, x2 * c + x1 * s], axis=-1).astype(x.dtype)


def _chunks(a):
    B, T, H = a.shape[:3]
    a = a.reshape((B, T // CHUNK, CHUNK, H) + a.shape[3:])
    return jnp.transpose(a, (1, 0, 3, 2) + tuple(range(4, a.ndim)))


def _unchunks(a):
    nC, B, H, C, d = a.shape
    return jnp.transpose(a, (1, 0, 3, 2, 4)).reshape(B, nC * C, H, d)


def _hgrn2_step(S, inp):
    q, k, v, lf = inp
    C = q.shape[-2]
    b = jnp.cumsum(lf, axis=-2)
    causal = jnp.tril(jnp.ones((C, C), dtype=bool))
    diff = b[..., :, None, :] - b[..., None, :, :]
    dec = jnp.exp(jnp.where(causal[:, :, None], diff, -jnp.inf))
    att = jnp.einsum('bhtd,bhsd,bhtsd->bhts', q, k, dec)
    o = (jnp.einsum('bhtd,bhde->bhte', q * jnp.exp(b), S)
         + jnp.einsum('bhts,bhse->bhte', att, v))
    b_last = b[..., -1:, :]
    S = (jnp.exp(b_last[..., 0, :])[..., None] * S
         + jnp.einsum('bhsd,bhse->bhde', k * jnp.exp(b_last - b), v))
    return S, o


def _hgrn2(q, f_pre, i, g, lb, norm_w):
    B, T, _ = q.shape
    f32 = jnp.float32
    qf = jax.nn.silu(q.astype(f32)).reshape(B, T, HG_HEADS, HG_DK)
    fp = f_pre.astype(f32)
    log_lb = jnp.log(jnp.maximum(lb, LB_FLOOR))
    logf = jnp.minimum(jax.nn.log_sigmoid(fp) + jax.nn.softplus(log_lb - fp), 0.0)
    kf = -jnp.expm1(logf)
    logf = logf.reshape(B, T, HG_HEADS, HG_DK)
    kf = kf.reshape(B, T, HG_HEADS, HG_DK)
    vf = i.astype(f32).reshape(B, T, HG_HEADS, HG_DV)
    S0 = jnp.zeros((B, HG_HEADS, HG_DK, HG_DV), f32)
    _, o = lax.scan(_hgrn2_step, S0, (_chunks(qf), _chunks(kf), _chunks(vf), _chunks(logf)))
    o = _rmsnorm(_unchunks(o), norm_w)
    o = o * jax.nn.silu(g.astype(f32)).reshape(B, T, HG_HEADS, HG_DV)
    return o.reshape(B, T, HG_HEADS * HG_DV).astype(q.dtype)


def _gdn_step(S, inp):
    qd, qk, u, w, kd, dl = inp
    v_new = u - jnp.einsum('bhck,bhkv->bhcv', w, S)
    o = jnp.einsum('bhck,bhkv->bhcv', qd, S) + jnp.einsum('bhts,bhsv->bhtv', qk, v_new)
    S = dl[..., None, None] * S + jnp.einsum('bhck,bhcv->bhkv', kd, v_new)
    return S, o


def _gdn(q, k, v, z, a, b, conv_w, A_log, dt_bias, norm_w):
    B, T, _ = q.shape
    f32 = jnp.float32
    qkv = jnp.concatenate([q, k, v], axis=-1)
    qkv = lax.conv_general_dilated(qkv, conv_w[:, None, :].astype(qkv.dtype), window_strides=(1,),
                                   padding=[(CONV_W - 1, 0)],
                                   dimension_numbers=('NWC', 'WIO', 'NWC'),
                                   feature_group_count=GD_CONV_C)
    qkv = jax.nn.silu(qkv.astype(f32))
    qf, kf, vf = jnp.split(qkv, [GD_W, 2 * GD_W], axis=-1)
    qf = qf.reshape(B, T, GD_HEADS, GD_DK)
    kf = kf.reshape(B, T, GD_HEADS, GD_DK)
    vf = vf.reshape(B, T, GD_HEADS, GD_DV)
    qf = qf * lax.rsqrt(jnp.sum(qf * qf, -1, keepdims=True) + NORM_EPS) * (GD_DK ** -0.5)
    kf = kf * lax.rsqrt(jnp.sum(kf * kf, -1, keepdims=True) + NORM_EPS)
    beta = jax.nn.sigmoid(b.astype(f32))
    g = -jnp.exp(A_log.astype(f32)) * jax.nn.softplus(a.astype(f32) + dt_bias.astype(f32))
    qc, kc, vc = _chunks(qf), _chunks(kf), _chunks(vf)
    bc = _chunks(beta[..., None])[..., 0]
    gc = jnp.cumsum(_chunks(g[..., None])[..., 0], axis=-1)
    C = CHUNK
    incl = jnp.tril(jnp.ones((C, C), dtype=bool))
    strict = jnp.tril(jnp.ones((C, C), dtype=bool), -1)
    L = jnp.exp(jnp.where(incl, gc[..., :, None] - gc[..., None, :], -jnp.inf))
    kk = jnp.einsum('nbhik,nbhjk->nbhij', kc, kc)
    M = jnp.where(strict, bc[..., :, None] * kk * L, 0.0)
    A = M + jnp.eye(C, dtype=f32)
    rhs = jnp.concatenate([vc * bc[..., None], kc * (bc * jnp.exp(gc))[..., None]], axis=-1)
    sol = lax.linalg.triangular_solve(A, rhs, left_side=True, lower=True, unit_diagonal=True)
    u, w = sol[..., :GD_DV], sol[..., GD_DV:]
    qk = jnp.where(incl, jnp.einsum('nbhik,nbhjk->nbhij', qc, kc) * L, 0.0)
    qd = qc * jnp.exp(gc)[..., None]
    g_last = gc[..., -1]
    kd = kc * jnp.exp(g_last[..., None] - gc)[..., None]
    dl = jnp.exp(g_last)
    S0 = jnp.zeros((B, GD_HEADS, GD_DK, GD_DV), f32)
    _, o = lax.scan(_gdn_step, S0, (qd, qk, u, w, kd, dl))
    o = _rmsnorm(_unchunks(o), norm_w)
    o = o * jax.nn.silu(z.astype(f32)).reshape(B, T, GD_HEADS, GD_DV)
    return o.reshape(B, T, GD_HEADS * GD_DV).astype(q.dtype)


def _nsa(q, kc, vc, ks, vs, kw, vw, gate, pos_k, w1k, w2k, pos_v, w1v, w2v, cos, sin):
    B, T = q.shape[:2]
    dt = q.dtype
    f32 = jnp.float32
    G, hpg, dh = NSA_KV, NSA_HPG, HEAD_DIM
    scale = dh ** -0.5
    tpos = jnp.arange(T)
    q = _rope(q.reshape(B, T, NSA_HEADS, dh), cos, sin)
    q = q.reshape(B, T, G, hpg, dh).transpose(0, 2, 3, 1, 4)

    def heads(a, rot):
        a = a.reshape(B, T, G, dh)
        if rot:
            a = _rope(a, cos, sin)
        return a.transpose(0, 2, 1, 3)

    kc, vc = heads(kc, True), heads(vc, False)
    ks, vs = heads(ks, True), heads(vs, False)
    kw, vw = heads(kw, True), heads(vw, False)

    n_cmp = (T - CMP_LEN) // CMP_STRIDE + 1
    cstart = jnp.arange(n_cmp) * CMP_STRIDE
    bidx = cstart[:, None] + jnp.arange(CMP_LEN)[None, :]

    def compress(a, pos, w1, w2):
        blk = (a[:, :, bidx] + pos).reshape(B, G, n_cmp, CMP_LEN * dh)
        return jax.nn.gelu(blk @ w1) @ w2

    k_cmp = compress(kc, pos_k, w1k, w2k)
    v_cmp = compress(vc, pos_v, w1v, w2v)
    s = jnp.einsum('bghtd,bgnd->bghtn', q, k_cmp).astype(f32) * scale
    cmask = (cstart + CMP_LEN - 1)[None, :] <= tpos[:, None]
    p_cmp = jnp.where(cmask, jax.nn.softmax(jnp.where(cmask, s, NEG), axis=-1), 0.0)
    o_cmp = jnp.einsum('bghtn,bgnd->bghtd', p_cmp.astype(dt), v_cmp)

    n_slc = T // SLC_LEN
    sstart = jnp.arange(n_slc) * SLC_LEN
    overlap = jnp.clip(jnp.minimum(cstart[:, None] + CMP_LEN, sstart[None, :] + SLC_LEN)
                       - jnp.maximum(cstart[:, None], sstart[None, :]), 0).astype(f32) / CMP_LEN
    imp = jnp.einsum('bghtn,nj->bgtj', p_cmp, overlap)
    blk = jnp.arange(n_slc)[None, :]
    cur = (tpos // SLC_LEN)[:, None]
    forced = (blk == 0) | (blk == cur) | (blk == cur - 1)
    imp = jnp.where(forced, jnp.inf, imp)
    imp = jnp.where(sstart[None, :] <= tpos[:, None], imp, -jnp.inf)
    n_top = min(SLC_TOP, n_slc)
    _, sel = lax.top_k(imp, n_top)
    ks_b = ks.reshape(B, G, n_slc, SLC_LEN, dh)
    vs_b = vs.reshape(B, G, n_slc, SLC_LEN, dh)
    nqc = T // SLC_Q_CHUNK
    q_c = jnp.moveaxis(q.reshape(B, G, hpg, nqc, SLC_Q_CHUNK, dh), 3, 0)
    sel_c = jnp.moveaxis(sel.reshape(B, G, nqc, SLC_Q_CHUNK, n_top), 2, 0)
    t_c = tpos.reshape(nqc, SLC_Q_CHUNK)
    bi = jnp.arange(B)[:, None, None, None]
    gi = jnp.arange(G)[None, :, None, None]

    def slc_block(args):
        qb, sb, tb = args
        kg = ks_b[bi, gi, sb]
        vg = vs_b[bi, gi, sb]
        sc = jnp.einsum('bghqd,bgqkld->bghqkl', qb, kg).astype(f32) * scale
        kpos = sb[..., None] * SLC_LEN + jnp.arange(SLC_LEN)
        m = kpos <= tb[None, None, :, None, None]
        sc = jnp.where(m[:, :, None], sc, NEG)
        shp = sc.shape
        p = jax.nn.softmax(sc.reshape(shp[:4] + (-1,)), axis=-1).reshape(shp)
        return jnp.einsum('bghqkl,bgqkld->bghqd', p.astype(dt), vg)

    o_slc = lax.map(slc_block, (q_c, sel_c, t_c))
    o_slc = jnp.moveaxis(o_slc, 0, 3).reshape(B, G, hpg, T, dh)

    nwb = T // WIN_BLOCK
    slab = jnp.arange(nwb)[:, None] * WIN_BLOCK + jnp.arange(WINDOW + WIN_BLOCK)[None, :]
    kw_b = jnp.pad(kw, ((0, 0), (0, 0), (WINDOW, 0), (0, 0)))[:, :, slab]
    vw_b = jnp.pad(vw, ((0, 0), (0, 0), (WINDOW, 0), (0, 0)))[:, :, slab]
    qw = q.reshape(B, G, hpg, nwb, WIN_BLOCK, dh)
    sw = jnp.einsum('bghnqd,bgnsd->bghnqs', qw, kw_b).astype(f32) * scale
    qpos = (jnp.arange(nwb)[:, None] * WIN_BLOCK + jnp.arange(WIN_BLOCK)[None, :])[:, :, None]
    kpos = (slab - WINDOW)[:, None, :]
    wmask = (kpos <= qpos) & (kpos > qpos - WINDOW) & (kpos >= 0)
    pw = jax.nn.softmax(jnp.where(wmask, sw, NEG), axis=-1)
    o_win = jnp.einsum('bghnqs,bgnsd->bghnqd', pw.astype(dt), vw_b).reshape(B, G, hpg, T, dh)

    gts = jax.nn.sigmoid(gate.astype(f32)).reshape(B, T, G, hpg, 3).transpose(0, 2, 3, 1, 4)
    gts = gts.astype(dt)
    o = gts[..., 0:1] * o_cmp + gts[..., 1:2] * o_slc + gts[..., 2:3] * o_win
    return o.transpose(0, 3, 1, 2, 4).reshape(B, T, NSA_W)


def setup_inputs(seed: int = 0) -> dict:
    key = jax.random.key(seed)
    ks = jax.random.split(key, 24)
    f32 = jnp.float32

    def nrm(k, shape, scale):
        return jax.random.normal(k, shape, f32) * scale

    dt = jnp.exp(jax.random.uniform(ks[14], (DEPTH, GD_HEADS), f32, math.log(1e-3), math.log(1e-1)))
    return {
        "x": nrm(ks[0], (BATCH, SEQ, D_MODEL), 1.0),
        "attn_norm": 1.0 + nrm(ks[1], (DEPTH, D_MODEL), 0.02),
        "w_in": nrm(ks[2], (DEPTH, D_MODEL, N_IN), D_MODEL ** -0.5),
        "w_out": nrm(ks[3], (DEPTH, MIX_W, D_MODEL), MIX_W ** -0.5),
        "ffn_norm": 1.0 + nrm(ks[4], (DEPTH, D_MODEL), 0.02),
        "w_gate": nrm(ks[5], (DEPTH, D_MODEL, FFN_HIDDEN), D_MODEL ** -0.5),
        "w_up": nrm(ks[6], (DEPTH, D_MODEL, FFN_HIDDEN), D_MODEL ** -0.5),
        "w_down": nrm(ks[7], (DEPTH, FFN_HIDDEN, D_MODEL), FFN_HIDDEN ** -0.5),
        "final_norm": 1.0 + nrm(ks[8], (D_MODEL,), 0.02),
        "hgrn_lb_logits": nrm(ks[9], (DEPTH, HG_W), 1.0),
        "hgrn_out_norm": 1.0 + nrm(ks[10], (DEPTH, HG_DV), 0.02),
        "gdn_conv": nrm(ks[11], (DEPTH, CONV_W, GD_CONV_C), CONV_W ** -0.5),
        "gdn_A_log": jnp.log(jax.random.uniform(ks[12], (DEPTH, GD_HEADS), f32, 1.0, 16.0)),
        "gdn_dt_bias": dt + jnp.log(-jnp.expm1(-dt)),
        "gdn_out_norm": 1.0 + nrm(ks[13], (DEPTH, GD_DV), 0.02),
        "cmp_pos_k": nrm(ks[15], (DEPTH, CMP_LEN, HEAD_DIM), 0.1),
        "cmp_w1_k": nrm(ks[16], (DEPTH, CMP_LEN * HEAD_DIM, CMP_HIDDEN), (CMP_LEN * HEAD_DIM) ** -0.5),
        "cmp_w2_k": nrm(ks[17], (DEPTH, CMP_HIDDEN, HEAD_DIM), CMP_HIDDEN ** -0.5),
        "cmp_pos_v": nrm(ks[18], (DEPTH, CMP_LEN, HEAD_DIM), 0.1),
        "cmp_w1_v": nrm(ks[19], (DEPTH, CMP_LEN * HEAD_DIM, CMP_HIDDEN), (CMP_LEN * HEAD_DIM) ** -0.5),
        "cmp_w2_v": nrm(ks[20], (DEPTH, CMP_HIDDEN, HEAD_DIM), CMP_HIDDEN ** -0.5),
    }


def reference(x, attn_norm, w_in, w_out, ffn_norm, w_gate, w_up, w_down, final_norm,
              hgrn_lb_logits, hgrn_out_norm, gdn_conv, gdn_A_log, gdn_dt_bias, gdn_out_norm,
              cmp_pos_k, cmp_w1_k, cmp_w2_k, cmp_pos_v, cmp_w1_v, cmp_w2_v):
    T = x.shape[1]
    cos, sin = _rope_tables(T)
    lb_p = jax.nn.softmax(hgrn_lb_logits.astype(jnp.float32), axis=0)
    lb_all = jnp.concatenate([jnp.zeros_like(lb_p[:1]), jnp.cumsum(lb_p, axis=0)[:-1]], axis=0)
    split_at = [int(v) for v in np.cumsum(IN_SIZES)[:-1]]
    for l in range(DEPTH):
        h = _rmsnorm(x, attn_norm[l])
        parts = jnp.split(h @ w_in[l], split_at, axis=-1)
        (hq, hf, hi, hg, gq, gk, gv, gz, ga, gb,
         nq, nkc, nvc, nks, nvs, nkw, nvw, ngate) = parts
        o_a = _hgrn2(hq, hf, hi, hg, lb_all[l], hgrn_out_norm[l])
        o_b = _gdn(gq, gk, gv, gz, ga, gb, gdn_conv[l], gdn_A_log[l], gdn_dt_bias[l], gdn_out_norm[l])
        o_c = _nsa(nq, nkc, nvc, nks, nvs, nkw, nvw, ngate,
                   cmp_pos_k[l], cmp_w1_k[l], cmp_w2_k[l], cmp_pos_v[l], cmp_w1_v[l], cmp_w2_v[l],
                   cos, sin)
        x = x + jnp.concatenate([o_a, o_b, o_c], axis=-1) @ w_out[l]
        h = _rmsnorm(x, ffn_norm[l])
        x = x + (jax.nn.silu(h @ w_gate[l]) * (h @ w_up[l])) @ w_down[l]
    return _rmsnorm(x, final_norm)
```

```python
import ml_dtypes
import numpy as np
from contextlib import ExitStack
import concourse.bass as bass
import concourse.mybir as mybir
from concourse.bass_utils import run_bass_kernel_spmd

F32 = mybir.dt.float32
BF16 = mybir.dt.bfloat16
I32 = mybir.dt.int32
U32 = mybir.dt.uint32
AF = mybir.ActivationFunctionType
ALU = mybir.AluOpType
AX = mybir.AxisListType

NDSEM = 12


class Prog:
    ENG = ('pe', 'dve', 'act', 'pool', 'sp')

    def __init__(self):
        self.nc = bass.Bass("TRN2", target_bir_lowering=False)
        self.es = ExitStack()
        self.ops = {e: [] for e in self.ENG}
        self.cnt = {e: 0 for e in self.ENG}
        self.dcnt = {}
        self.dnext = {e: 0 for e in self.ENG}
        self.seen = {e: {} for e in self.ENG}
        self.lastw = {}
        self.readers = {}
        self.nbuf = 0

    def dram_in(self, name, shape, dtype=F32):
        return self.nc.dram_tensor(name, list(shape), dtype, kind="ExternalInput").ap()

    def dram_out(self, name, shape, dtype=F32):
        return self.nc.dram_tensor(name, list(shape), dtype, kind="ExternalOutput").ap()

    def dram_tmp(self, name, shape, dtype=F32):
        return self.nc.dram_tensor(name, list(shape), dtype, kind="Internal").ap()

    def sbuf(self, name, shape, dtype=F32):
        self.nbuf += 1
        return self.es.enter_context(self.nc.sbuf_tensor("sb_" + name, list(shape), dtype))

    def psum(self, name, shape, dtype=F32):
        return self.es.enter_context(self.nc.psum_tensor("pp_" + name, list(shape), dtype))

    def _need(self, eng, waits, comp, kind):
        if comp is None:
            return
        semkey, val, src = comp
        if src == eng and semkey == eng:
            if eng == 'pe' or kind == 'war':
                return
        if self.seen[eng].get(semkey, 0) >= val:
            return
        if waits.get(semkey, 0) < val:
            waits[semkey] = val

    def op(self, eng, fn, reads=(), writes=(), dma=False):
        waits = {}
        for t in reads:
            self._need(eng, waits, self.lastw.get(t), 'raw')
        for t in writes:
            self._need(eng, waits, self.lastw.get(t), 'waw')
            for comp in self.readers.get(t, {}).values():
                self._need(eng, waits, comp, 'war')
        for k, v in waits.items():
            self.seen[eng][k] = v
        if dma:
            i = self.dnext[eng] % NDSEM
            self.dnext[eng] += 1
            key = ('d', eng, i)
            self.dcnt[key] = self.dcnt.get(key, 0) + 1
            comp = (key, 16 * self.dcnt[key], eng)
            inc = (key, 16)
        else:
            self.cnt[eng] += 1
            comp = (eng, self.cnt[eng], eng)
            inc = (eng, 1)
        for t in writes:
            self.lastw[t] = comp
            self.readers[t] = {}
        for t in reads:
            self.readers.setdefault(t, {})[comp[0]] = comp
        self.ops[eng].append((list(waits.items()), fn, inc))

    def dma(self, out, in_, reads=(), writes=(), q='sp', **kw):
        self.op(q, lambda e: e.dma_start(out=out, in_=in_, **kw), reads, writes, dma=True)

    def mm(self, out, lhsT, rhs, start, stop, reads=(), writes=()):
        self.op('pe', lambda e: e.matmul(out, lhsT, rhs, start=start, stop=stop), reads, writes)

    def transpose(self, out, in_, ident, reads=(), writes=()):
        self.op('pe', lambda e: e.transpose(out, in_, ident), reads, writes)

    def act(self, out, in_, func, reads=(), writes=(), **kw):
        self.op('act', lambda e: e.activation(out=out, in_=in_, func=func, **kw), reads, writes)

    def v(self, eng, name, *args, reads=(), writes=(), **kw):
        self.op(eng, lambda e: getattr(e, name)(*args, **kw), reads, writes)

    def finish(self, out_tokens=()):
        nc = self.nc
        sems = {}
        for e in ('pe', 'dve', 'act', 'pool'):
            sems[e] = self.es.enter_context(nc.semaphore("s_" + e))
        for key in self.dcnt:
            sems[key] = self.es.enter_context(nc.semaphore("d_%s_%d" % (key[1], key[2])))
        final_waits = [(key, 16 * n) for key, n in self.dcnt.items()]
        ops = self.ops
        nops = {e: len(ops[e]) for e in self.ENG}

        def emit(e, eng_name):
            for waits, fn, inc in ops[eng_name]:
                for k, v in waits:
                    e.wait_ge(sems[k], v)
                ins = fn(e)
                ins.then_inc(sems[inc[0]], inc[1])
            if eng_name == 'sp':
                for k, v in final_waits:
                    e.wait_ge(sems[k], v)

        with nc.Block() as block:
            @block.sync
            def _(e):
                emit(e, 'sp')

            @block.tensor
            def _(e):
                emit(e, 'pe')

            @block.vector
            def _(e):
                emit(e, 'dve')

            @block.scalar
            def _(e):
                emit(e, 'act')

            @block.gpsimd
            def _(e):
                emit(e, 'pool')
        self.es.close()
        return nops


EPS = 1e-6


def rstd_from_ss(p, ss, D):
    p.v('dve', 'tensor_scalar', ss[:, 1:2], ss[:, 0:1], 1.0 / D, EPS, ALU.mult, ALU.add,
        reads=['rms_ss'], writes=['rms_ss1'])
    p.act(ss[:, 1:2], ss[:, 1:2], AF.Sqrt, reads=['rms_ss1'], writes=['rms_ss1'])
    p.v('dve', 'reciprocal', ss[:, 0:1], ss[:, 1:2], reads=['rms_ss1'], writes=['rms_ss'])


def rms_to_hT(p, xt, tokx, D, nw, hT, col0, tokh, ident, scr, ss, banks):
    KC = D // 128
    p.v('dve', 'memset', ss[:, 0:1], 0.0, writes=['rms_ss'])
    p.act(scr[:, :], xt[:, :], AF.Square, reads=[tokx, 'rms_ss'], writes=['rms_scr', 'rms_ss'], accum_out=ss[:, 0:1])
    rstd_from_ss(p, ss, D)
    p.v('dve', 'tensor_scalar', scr[:, :], xt[:, :], ss[:, 0:1], None, ALU.mult,
        reads=[tokx, 'rms_ss'], writes=['rms_scr'])
    for g in range(0, KC, 4):
        bt, btok = banks[(g // 4) % len(banks)]
        n = min(4, KC - g)
        for j in range(n):
            kc = g + j
            p.transpose(bt[:, j * 128:(j + 1) * 128], scr[:, kc * 128:(kc + 1) * 128], ident[:, :],
                        reads=['rms_scr', 'ident'], writes=[btok])
        for j in range(n):
            kc = g + j
            eng = 'dve' if (j % 2 == 0) else 'pool'
            if eng == 'pool':
                p.act(hT[:, kc, col0:col0 + 128], bt[:, j * 128:(j + 1) * 128], AF.Copy,
                      reads=[btok, 'nw'], writes=[tokh], scale=nw[:, kc:kc + 1])
            else:
                p.v('dve', 'tensor_scalar', hT[:, kc, col0:col0 + 128], bt[:, j * 128:(j + 1) * 128],
                    nw[:, kc:kc + 1], None, ALU.mult, reads=[btok, 'nw'], writes=[tokh])


def build_p5(NTOK, D, F, last, G=256, wdt=F32):
    p = Prog()
    KC = D // 128
    FC = F // 128
    NB = D // 512
    TT = G // 128
    x = p.dram_in("x", [NTOK, D])
    mixT = p.dram_in("mixT", [D, NTOK], BF16)
    w_out = p.dram_in("w_out", [D, D], wdt)
    w_gate = p.dram_in("w_gate", [D, F], wdt)
    w_up = p.dram_in("w_up", [D, F], wdt)
    w_down = p.dram_in("w_down", [F, D], wdt)
    ffn_norm = p.dram_in("ffn_norm", [D])
    final_norm = p.dram_in("final_norm", [D])
    ident_d = p.dram_in("ident", [128, 128])
    y = p.dram_out("y", [NTOK, D])

    wq = 'pool' if wdt == F32 else 'sp'

    ident = p.sbuf("ident", [128, 128])
    nw = p.sbuf("nw", [128, KC])
    fw_b = p.sbuf("fw_b", [128, D])
    actA = p.sbuf("actA", [128, KC, G], BF16)
    x1 = [p.sbuf("x1_%d" % i, [128, D]) for i in range(TT)]
    actT = p.sbuf("actT", [128, FC, G], BF16)
    scr = p.sbuf("scr", [128, D])
    ss = p.sbuf("ss", [128, 2])
    NWB = 4
    WP = 4
    wbuf = [p.sbuf("wb_%d" % i, [128, WP, 512], BF16) for i in range(NWB)]
    NGB = 2
    gbuf = [p.sbuf("wg_%d" % i, [128, KC, 128], BF16) for i in range(NGB)]
    ubuf = [p.sbuf("wu_%d" % i, [128, KC, 128], BF16) for i in range(NGB)]
    sil = [p.sbuf("sil_%d" % i, [128, G]) for i in range(2)]
    banks = [(p.psum("ps%d" % i, [128, 512]), 'ps%d' % i) for i in range(8)]

    p.dma(ident[:, :], ident_d[:, :], writes=['ident'])
    p.dma(nw[:, :], ffn_norm.rearrange("(c p) -> p c", p=128), writes=['nw'], allow_slow_non_contiguous=True)
    if last:
        p.dma(fw_b[:, :], final_norm.partition_broadcast(128), writes=['fw_b'])

    wi = 0
    gi = 0
    for g0 in range(0, NTOK, G):
        p.dma(actA[:, :, :], mixT[:, g0:g0 + G].rearrange("(c p) t -> p c t", p=128),
              writes=['actA'])
        for tt in range(TT):
            p.dma(x1[tt][:, :], x[g0 + tt * 128:g0 + (tt + 1) * 128, :], writes=['x1_%d' % tt], q='act')
        for nb in range(NB):
            accs = [banks[(nb % 2) * TT + tt] for tt in range(TT)]
            for k0 in range(0, KC, WP):
                kn = min(WP, KC - k0)
                wb = wbuf[wi % NWB]
                wtok = 'wb_%d' % (wi % NWB)
                wi += 1
                p.dma(wb[:, 0:kn, :],
                      w_out[k0 * 128:(k0 + kn) * 128, nb * 512:(nb + 1) * 512].rearrange("(c p) n -> p c n", p=128),
                      writes=[wtok], q=wq)
                for tt in range(TT):
                    bt, btok = accs[tt]
                    for j in range(kn):
                        kc = k0 + j
                        p.mm(bt[:, :], actA[:, kc, tt * 128:(tt + 1) * 128], wb[:, j, :],
                             kc == 0, kc == KC - 1, reads=['actA', wtok], writes=[btok])
            for tt in range(TT):
                bt, btok = accs[tt]
                p.v('dve', 'tensor_tensor', x1[tt][:, nb * 512:(nb + 1) * 512], bt[:, :],
                    x1[tt][:, nb * 512:(nb + 1) * 512], ALU.add,
                    reads=[btok, 'x1_%d' % tt], writes=['x1_%d' % tt])
        for tt in range(TT):
            rms_to_hT(p, x1[tt], 'x1_%d' % tt, D, nw, actA, tt * 128, 'actA', ident, scr, ss,
                      banks[4:8])
        for fc in range(FC):
            gb = gbuf[gi % NGB]
            ub = ubuf[gi % NGB]
            gtok = 'wg_%d' % (gi % NGB)
            utok = 'wu_%d' % (gi % NGB)
            pg, pgtok = banks[4 + (gi % 2) * 2]
            pu, putok = banks[5 + (gi % 2) * 2]
            sl = sil[gi % 2]
            sltok = 'sil_%d' % (gi % 2)
            gi += 1
            for k0 in range(0, KC, 8):
                kn = min(8, KC - k0)
                p.dma(gb[:, k0:k0 + kn, :], w_gate[k0 * 128:(k0 + kn) * 128, fc * 128:(fc + 1) * 128].rearrange("(c p) f -> p c f", p=128),
                      writes=[gtok], q=wq)
                p.dma(ub[:, k0:k0 + kn, :], w_up[k0 * 128:(k0 + kn) * 128, fc * 128:(fc + 1) * 128].rearrange("(c p) f -> p c f", p=128),
                      writes=[utok], q=wq)
            for kc in range(KC):
                p.mm(pg[:, 0:G], gb[:, kc, :], actA[:, kc, :], kc == 0, kc == KC - 1,
                     reads=[gtok, 'actA'], writes=[pgtok])
            for kc in range(KC):
                p.mm(pu[:, 0:G], ub[:, kc, :], actA[:, kc, :], kc == 0, kc == KC - 1,
                     reads=[utok, 'actA'], writes=[putok])
            p.act(sl[:, :], pg[:, 0:G], AF.Silu, reads=[pgtok], writes=[sltok])
            p.v('dve', 'tensor_tensor', actT[:, fc, :], sl[:, :], pu[:, 0:G], ALU.mult,
                reads=[sltok, putok], writes=['actT'])
        for nb in range(NB):
            accs = [banks[(nb % 2) * TT + tt] for tt in range(TT)]
            for f0 in range(0, FC, WP):
                fn = min(WP, FC - f0)
                wb = wbuf[wi % NWB]
                wtok = 'wb_%d' % (wi % NWB)
                wi += 1
                p.dma(wb[:, 0:fn, :],
                      w_down[f0 * 128:(f0 + fn) * 128, nb * 512:(nb + 1) * 512].rearrange("(c p) n -> p c n", p=128),
                      writes=[wtok], q=wq)
                for tt in range(TT):
                    bt, btok = accs[tt]
                    for j in range(fn):
                        fc = f0 + j
                        p.mm(bt[:, :], actT[:, fc, tt * 128:(tt + 1) * 128], wb[:, j, :],
                             fc == 0, fc == FC - 1, reads=['actT', wtok], writes=[btok])
            for tt in range(TT):
                bt, btok = accs[tt]
                p.v('dve', 'tensor_tensor', x1[tt][:, nb * 512:(nb + 1) * 512], bt[:, :],
                    x1[tt][:, nb * 512:(nb + 1) * 512], ALU.add,
                    reads=[btok, 'x1_%d' % tt], writes=['x1_%d' % tt])
        for tt in range(TT):
            xt = x1[tt]
            tok = 'x1_%d' % tt
            if last:
                p.v('dve', 'memset', ss[:, 0:1], 0.0, writes=['rms_ss'])
                p.act(scr[:, :], xt[:, :], AF.Square, reads=[tok, 'rms_ss'], writes=['rms_scr', 'rms_ss'],
                      accum_out=ss[:, 0:1])
                rstd_from_ss(p, ss, D)
                p.v('dve', 'scalar_tensor_tensor', xt[:, :], xt[:, :], ss[:, 0:1], fw_b[:, :],
                    ALU.mult, ALU.mult, reads=[tok, 'rms_ss', 'fw_b'], writes=[tok])
            p.dma(y[g0 + tt * 128:g0 + (tt + 1) * 128, :], xt[:, :], reads=[tok], writes=['y'], q='act')
    nops = p.finish()
    return p.nc, nops


def build_p1(NTOK, D, NOUT, G=512, wdt=F32):
    p = Prog()
    KC = D // 128
    G = min(G, NTOK)
    TT = G // 128
    x = p.dram_in("x", [NTOK, D])
    w = p.dram_in("w", [D, NOUT], wdt)
    norm = p.dram_in("norm", [D])
    ident_d = p.dram_in("ident", [128, 128])
    projT = p.dram_out("projT", [NOUT, NTOK])
    wq = 'pool' if wdt == F32 else 'sp'
    ident = p.sbuf("ident", [128, 128])
    nw = p.sbuf("nw", [128, KC])
    hT = p.sbuf("hT", [128, KC, G], BF16)
    xt = [p.sbuf("xt_%d" % i, [128, D]) for i in range(2)]
    scr = p.sbuf("scr", [128, D])
    ss = p.sbuf("ss", [128, 2])
    NWB = 3
    wbuf = [p.sbuf("wb_%d" % i, [128, KC, 128], BF16) for i in range(NWB)]
    NOB = 4
    obuf = [p.sbuf("ob_%d" % i, [128, G]) for i in range(NOB)]
    banks = [(p.psum("ps%d" % i, [128, 512]), 'ps%d' % i) for i in range(8)]
    p.dma(ident[:, :], ident_d[:, :], writes=['ident'])
    p.dma(nw[:, :], norm.rearrange("(c p) -> p c", p=128), writes=['nw'], allow_slow_non_contiguous=True)
    wi = 0
    blocks = [(c0, min(128, NOUT - c0)) for c0 in range(0, NOUT, 128)]
    for g0 in range(0, NTOK, G):
        for tt in range(TT):
            xb = xt[tt % 2]
            tok = 'xt_%d' % (tt % 2)
            p.dma(xb[:, :], x[g0 + tt * 128:g0 + (tt + 1) * 128, :], writes=[tok], q='act')
            rms_to_hT(p, xb, tok, D, nw, hT, tt * 128, 'hT', ident, scr, ss, banks[4:8])
        for (c0, cw) in blocks:
            wb = wbuf[wi % NWB]
            wtok = 'wb_%d' % (wi % NWB)
            ob = obuf[wi % NOB]
            otok = 'ob_%d' % (wi % NOB)
            bt, btok = banks[wi % 4]
            wi += 1
            for k0 in range(0, KC, 8):
                kn = min(8, KC - k0)
                p.dma(wb[:, k0:k0 + kn, 0:cw],
                      w[k0 * 128:(k0 + kn) * 128, c0:c0 + cw].rearrange("(c p) f -> p c f", p=128),
                      writes=[wtok], q=wq)
            for kc in range(KC):
                p.mm(bt[0:cw, 0:G], wb[:, kc, 0:cw], hT[:, kc, :], kc == 0, kc == KC - 1,
                     reads=[wtok, 'hT'], writes=[btok])
            if wi % 2 == 0:
                p.act(ob[0:cw, :], bt[0:cw, 0:G], AF.Copy, reads=[btok], writes=[otok])
            else:
                p.v('dve', 'tensor_copy', ob[0:cw, :], bt[0:cw, 0:G], reads=[btok], writes=[otok])
            p.dma(projT[c0:c0 + cw, g0:g0 + G], ob[0:cw, :], reads=[otok], writes=['projT'])
    nops = p.finish()
    return p.nc, nops


def cumsum_chunks(p, A, B, atok, btok, NC, C=64, P=128):
    src, dst, stok, dtok = A, B, atok, btok
    s = 1
    while s < C:
        sv = src[0:P, :].rearrange("p (c s) -> p c s", s=C)
        dv = dst[0:P, :].rearrange("p (c s) -> p c s", s=C)
        p.v('pool', 'tensor_copy', dv[:, :, 0:s], sv[:, :, 0:s], reads=[stok], writes=[dtok])
        p.v('dve', 'tensor_tensor', dv[:, :, s:C], sv[:, :, s:C], sv[:, :, 0:C - s], ALU.add,
            reads=[stok], writes=[dtok])
        src, dst, stok, dtok = dst, src, dtok, stok
        s *= 2
    return src, stok


def build_hgrn(NH, T, layer, stop=99):
    p = Prog()
    C = 64
    NC = T // C
    NP = T // 128
    qT = p.dram_in("qT", [NH * 128, T]); fT = p.dram_in("fT", [NH * 128, T])
    iT = p.dram_in("iT", [NH * 128, T]); gT = p.dram_in("gT", [NH * 128, T])
    lbl = p.dram_in("lbl", [2, NH * 128]); onw_d = p.dram_in("onw", [128])
    identb_d = p.dram_in("identb", [128, 128], BF16)
    mask_d = p.dram_in("maskbd", [128, 128]); ones_d = p.dram_in("ones", [128, 128])
    halfm_d = p.dram_in("halfm", [128, 2])
    oT = p.dram_out("oT", [NH * 128, T], BF16)

    identb = p.sbuf("identb", [128, 128], BF16); mask = p.sbuf("mask", [128, 128]); ones = p.sbuf("ones", [128, 128])
    onw = p.sbuf("onw", [128, 1]); lb = p.sbuf("lb", [128, 4])
    tq = p.sbuf("tq", [128, T]); tf = p.sbuf("tf", [128, T]); ti = p.sbuf("ti", [128, T]); tg = p.sbuf("tg", [128, T])
    tb = p.sbuf("tb", [128, T]); tk = p.sbuf("tk", [128, T]); td = p.sbuf("td", [128, T]); te = p.sbuf("te", [128, T])
    qe = p.sbuf("qe", [128, T], BF16); ke = p.sbuf("ke", [128, T], BF16)
    qd = p.sbuf("qd", [128, T], BF16); kd = p.sbuf("kd", [128, T], BF16); vb = p.sbuf("vb", [128, T], BF16)
    vtok = p.sbuf("vtok", [128, NP, 128], BF16); kdtok = p.sbuf("kdtok", [128, NP, 128], BF16)
    att = p.sbuf("att", [128, NP, 128], BF16)
    kdh = [p.sbuf("kdh%d" % i, [128, NP, 128], BF16) for i in range(2)]
    halfm = p.sbuf("halfm", [128, 2])
    Sbf = p.sbuf("Sbf", [128, NC, 128], BF16)
    S32 = [p.sbuf("S32_%d" % i, [128, 128]) for i in range(2)]
    dl = p.sbuf("dl", [128, NC])
    o2 = p.sbuf("o2", [128, 512]); rs = p.sbuf("rs", [128, 512]); t1 = p.sbuf("t1", [128, 512])
    ob = p.sbuf("ob", [128, T], BF16)
    banks = [(p.psum("ps%d" % i, [128, 512]), 'ps%d' % i) for i in range(6)]
    bankb = [(p.psum("pb%d" % i, [128, 1024], BF16), 'pb%d' % i) for i in range(2)]

    p.dma(identb[:, :], identb_d[:, :], writes=['identb'])
    p.dma(mask[:, :], mask_d[:, :], writes=['mask'])
    p.dma(ones[:, :], ones_d[:, :], writes=['ones'])
    p.dma(halfm[:, :], halfm_d[:, :], writes=['halfm'])
    p.dma(onw[:, :], onw_d.rearrange("(p o) -> p o", o=1), writes=['onw'])

    def V3(t):
        return t[:, :].rearrange("p (c s) -> p c s", s=C)

    for h in range(NH):
        rows = slice(h * 128, (h + 1) * 128)
        p.dma(tq[:, :], qT[rows, :], writes=['tq'])
        p.dma(tf[:, :], fT[rows, :], writes=['tf'])
        p.dma(ti[:, :], iT[rows, :], writes=['ti'], q='act')
        p.dma(tg[:, :], gT[rows, :], writes=['tg'], q='act')
        if layer == 0:
            p.v('dve', 'memset', lb[:, 0:1], 0.0, writes=['lb'])
            p.v('dve', 'memset', lb[:, 1:2], 1.0, writes=['lb'])
        else:
            p.dma(lb[:, 2:4], lbl[:, rows].rearrange("l p -> p l"), writes=['lb'], allow_slow_non_contiguous=True)
            p.v('dve', 'tensor_tensor', lb[:, 2:3], lb[:, 2:3], lb[:, 3:4], ALU.subtract, reads=['lb'], writes=['lb'])
            p.act(lb[:, 0:1], lb[:, 2:3], AF.Sigmoid, reads=['lb'], writes=['lb'])
            p.v('dve', 'tensor_scalar', lb[:, 1:2], lb[:, 0:1], -1.0, 1.0, ALU.mult, ALU.add, reads=['lb'], writes=['lb'])
        p.act(tq[:, :], tq[:, :], AF.Silu, reads=['tq'], writes=['tq'])
        p.act(tf[:, :], tf[:, :], AF.Sigmoid, reads=['tf'], writes=['tf'])
        p.v('dve', 'tensor_scalar', tf[:, :], tf[:, :], lb[:, 1:2], lb[:, 0:1], ALU.mult, ALU.add,
            reads=['tf', 'lb'], writes=['tf'])
        p.v('pool', 'tensor_scalar', tk[:, :], tf[:, :], -1.0, 1.0, ALU.mult, ALU.add, reads=['tf'], writes=['tk'])
        p.act(tb[:, :], tf[:, :], AF.Ln, reads=['tf'], writes=['tb'])
        bres, btok = cumsum_chunks(p, tb, te, 'tb', 'te', NC)
        assert bres is tb
        b3 = V3(tb)
        p.act(dl[:, :], b3[:, :, C - 1:C].rearrange("p c o -> p (c o)"), AF.Exp, reads=['tb'], writes=['dl'])
        p.v('dve', 'tensor_tensor', V3(td), b3, b3[:, :, C // 2 - 1:C // 2].to_broadcast([128, NC, C]), ALU.subtract,
            reads=['tb'], writes=['td'])
        p.act(te[:, :], td[:, :], AF.Exp, reads=['td'], writes=['te'])
        p.v('dve', 'tensor_tensor', qe[:, :], tq[:, :], te[:, :], ALU.mult, reads=['tq', 'te'], writes=['qe'])
        p.act(te[:, :], td[:, :], AF.Exp, reads=['td'], writes=['te'], scale=-1.0)
        p.v('dve', 'tensor_tensor', ke[:, :], tk[:, :], te[:, :], ALU.mult, reads=['tk', 'te'], writes=['ke'])
        p.act(te[:, :], tb[:, :], AF.Exp, reads=['tb'], writes=['te'])
        p.v('dve', 'tensor_tensor', qd[:, :], tq[:, :], te[:, :], ALU.mult, reads=['tq', 'te'], writes=['qd'])
        p.v('dve', 'tensor_tensor', V3(td), b3, b3[:, :, C - 1:C].to_broadcast([128, NC, C]), ALU.subtract,
            reads=['tb'], writes=['td'])
        p.act(te[:, :], td[:, :], AF.Exp, reads=['td'], writes=['te'], scale=-1.0)
        p.v('dve', 'tensor_tensor', kd[:, :], tk[:, :], te[:, :], ALU.mult, reads=['tk', 'te'], writes=['kd'])
        p.v('pool', 'tensor_copy', vb[:, :], ti[:, :], reads=['ti'], writes=['vb'])
        p.act(tg[:, :], tg[:, :], AF.Silu, reads=['tg'], writes=['tg'])
        if stop <= 1:
            continue
        for src, stok, dst, dtok in ((vb, 'vb', vtok, 'vtok'), (kd, 'kd', kdtok, 'kdtok')):
            for j0 in range(0, NP, 8):
                bt, btk = bankb[(j0 // 8) % 2]
                jn = min(8, NP - j0)
                for j in range(jn):
                    p.transpose(bt[:, j * 128:(j + 1) * 128], src[:, (j0 + j) * 128:(j0 + j + 1) * 128], identb[:, :],
                                reads=[stok, 'identb'], writes=[btk])
                p.v('dve', 'tensor_copy', dst[:, j0:j0 + jn, :],
                    bt[:, 0:jn * 128].rearrange("p (j e) -> p j e", e=128), reads=[btk], writes=[dtok])
        if stop <= 2:
            continue
        for j0 in range(0, NP, 4):
            bt, btk = banks[(j0 // 4) % 2]
            jn = min(4, NP - j0)
            for j in range(jn):
                cs = slice((j0 + j) * 128, (j0 + j + 1) * 128)
                p.mm(bt[:, j * 128:(j + 1) * 128], ke[:, cs], qe[:, cs], True, True, reads=['ke', 'qe'], writes=[btk])
            p.v('dve', 'tensor_tensor', att[:, j0:j0 + jn, :],
                bt[:, 0:jn * 128].rearrange("p (j e) -> p j e", e=128),
                mask[:, :].rearrange("p (o e) -> p o e", o=1).to_broadcast([128, jn, 128]), ALU.mult,
                reads=[btk, 'mask'], writes=['att'])
        if stop <= 3:
            continue
        for hh in range(2):
            p.v('pool', 'tensor_scalar', kdh[hh][:, :, :], kdtok[:, :, :], halfm[:, hh:hh + 1], None, ALU.mult,
                reads=['kdtok', 'halfm'], writes=['kdh'])
        p.v('dve', 'memset', S32[0][:, :], 0.0, writes=['S32_0'])
        p.v('pool', 'memset', Sbf[:, 0, :], 0.0, writes=['Sbf'])
        for c0 in range(0, NC - 1, 4):
            bt, btk = banks[2 + (c0 // 4) % 2]
            cn = min(4, NC - 1 - c0)
            for j in range(cn):
                c = c0 + j
                p.mm(bt[:, j * 128:(j + 1) * 128], kdh[c % 2][:, c // 2, :], vtok[:, c // 2, :], True, True,
                     reads=['kdh', 'vtok'], writes=[btk])
            for j in range(cn):
                c = c0 + j
                sa, sb = S32[c % 2], S32[(c + 1) % 2]
                p.v('dve', 'scalar_tensor_tensor', sb[:, :], sa[:, :], dl[:, c:c + 1], bt[:, j * 128:(j + 1) * 128],
                    ALU.mult, ALU.add, reads=['S32_%d' % (c % 2), 'dl', btk], writes=['S32_%d' % ((c + 1) % 2)])
                p.act(Sbf[:, c + 1, :], sb[:, :], AF.Copy, reads=['S32_%d' % ((c + 1) % 2)], writes=['Sbf'])
        if stop <= 4:
            continue
        for q0 in range(0, T, 512):
            qn = min(512, T - q0)
            bo, botok = banks[4]
            bs, bstok = banks[5]
            for j in range(qn // 128):
                jj = q0 // 128 + j
                cs = slice(j * 128, (j + 1) * 128)
                p.mm(bo[:, cs], vtok[:, jj, :], att[:, jj, :], True, False, reads=['vtok', 'att'], writes=[botok])
                for cc in range(2):
                    c = 2 * jj + cc
                    p.mm(bo[:, j * 128 + cc * 64:j * 128 + cc * 64 + 64], Sbf[:, c, :],
                         qd[:, c * 64:(c + 1) * 64], False, cc == 1, reads=['Sbf', 'qd'], writes=[botok])
            p.act(o2[:, 0:qn], bo[:, 0:qn], AF.Square, reads=[botok], writes=['o2'])
            p.mm(bs[:, 0:qn], ones[:, :], o2[:, 0:qn], True, True, reads=['ones', 'o2'], writes=[bstok])
            p.v('dve', 'tensor_scalar', rs[:, 0:qn], bs[:, 0:qn], 1.0 / 128, EPS, ALU.mult, ALU.add,
                reads=[bstok], writes=['rs'])
            p.act(rs[:, 0:qn], rs[:, 0:qn], AF.Sqrt, reads=['rs'], writes=['rs'])
            p.v('dve', 'reciprocal', rs[:, 0:qn], rs[:, 0:qn], reads=['rs'], writes=['rs'])
            p.v('dve', 'tensor_tensor', t1[:, 0:qn], bo[:, 0:qn], rs[:, 0:qn], ALU.mult, reads=[botok, 'rs'], writes=['t1'])
            p.v('dve', 'scalar_tensor_tensor', ob[:, q0:q0 + qn], t1[:, 0:qn], onw[:, 0:1], tg[:, q0:q0 + qn],
                ALU.mult, ALU.mult, reads=['t1', 'onw', 'tg'], writes=['ob'])
        p.dma(oT[rows, :], ob[:, :], reads=['ob'], writes=['oT'])
    nops = p.finish()
    return p.nc, nops


SKIP = ''


def build_gdn(NH, T, stop=99):
    p = Prog()
    C = 64
    NC = T // C
    NP = T // 128
    qT = p.dram_in("qT", [NH * 128, T]); kT = p.dram_in("kT", [NH * 128, T])
    vT = p.dram_in("vT", [NH * 128, T]); zT = p.dram_in("zT", [NH * 128, T])
    aT = p.dram_in("aT", [NH, T]); bT = p.dram_in("bT", [NH, T])
    cw_d = p.dram_in("cw", [NH, 3, 128, 4]); Alog = p.dram_in("Alog", [NH]); dtb = p.dram_in("dtb", [NH])
    onw_d = p.dram_in("onw", [128])
    ident_d = p.dram_in("ident", [128, 128]); ones_d = p.dram_in("ones", [128, 128])
    mU_d = p.dram_in("maskU", [128, 128]); mUs_d = p.dram_in("maskUs", [128, 128]); mLs_d = p.dram_in("maskLs", [128, 128])
    halfm_d = p.dram_in("halfm", [128, 2])
    oT = p.dram_out("oT", [NH * 128, T], BF16)

    ident = p.sbuf("ident", [128, 128]); ones = p.sbuf("ones", [128, 128])
    mU = p.sbuf("mU", [128, 128]); mUs = p.sbuf("mUs", [128, 128]); mLs = p.sbuf("mLs", [128, 128])
    halfm = p.sbuf("halfm", [128, 2]); onw = p.sbuf("onw", [128, 1])
    cw = p.sbuf("cw", [128, 3, 4]); sc = p.sbuf("sc", [128, 4])
    identb_d = p.dram_in("identb", [128, 128], BF16)
    identb = p.sbuf("identb", [128, 128], BF16)
    tx1 = p.sbuf("tx", [128, T])
    tx = [tx1, tx1, tx1]
    ty = [p.sbuf("ty%d" % i, [128, T]) for i in range(3)]
    tz = p.sbuf("tz", [128, T])
    tt = p.sbuf("tt", [128, T])
    Gb = p.sbuf("Gb", [128, T]); Bb = p.sbuf("Bb", [128, T]); Eb = p.sbuf("Eb", [128, T])
    tt2 = p.sbuf("tt2", [128, T])
    colsH = p.sbuf("colsH", [128, 2, NP])
    cols = p.sbuf("cols", [128, 4, NP])
    qb = p.sbuf("qb", [128, T], BF16); kb = p.sbuf("kb", [128, T], BF16); qdb = p.sbuf("qdb", [128, T], BF16)
    kdh = [p.sbuf("kdh%d" % i, [128, NP, 128], BF16) for i in range(2)]
    D1 = ty[0][:, :].rearrange("p (j e) -> p j e", e=128)
    Pm = ty[1][:, :].rearrange("p (j e) -> p j e", e=128)
    Qm = ty[2][:, :].rearrange("p (j e) -> p j e", e=128)
    NG = (NP + 3) // 4
    AL = {0: ['D1'], 1: [('P', g) for g in range(NG)], 2: [('Q', g) for g in range(NG)]}
    Y = p.sbuf("Y", [128, NP, 256])
    qkT = p.sbuf("qkT", [128, NP, 128], BF16)
    wtokb = p.sbuf("wtokb", [128, NP, 128], BF16); wTb = p.sbuf("wTb", [128, T], BF16)
    vnew = p.sbuf("vnew", [128, NC, 128], BF16)
    Sbf = p.sbuf("Sbf", [128, NC, 128], BF16)
    S32 = [p.sbuf("S32_%d" % i, [128, 128]) for i in range(2)]
    o2 = p.sbuf("o2", [128, 512]); rs = p.sbuf("rs", [128, 512]); t1 = p.sbuf("t1", [128, 512])
    ob = p.sbuf("ob", [128, T], BF16)
    banks = [(p.psum("ps%d" % i, [128, 512]), 'ps%d' % i) for i in range(7)]
    bankb = [(p.psum("pb%d" % i, [128, 1024], BF16), 'pb%d' % i) for i in range(1)]
    bi = [0]

    def nb():
        b = banks[bi[0] % len(banks)]
        bi[0] += 1
        return b

    for dst, src, tok in ((ident, ident_d, 'ident'), (ones, ones_d, 'ones'), (mU, mU_d, 'mU'), (mUs, mUs_d, 'mUs'),
                          (mLs, mLs_d, 'mLs'), (halfm, halfm_d, 'halfm'), (identb, identb_d, 'identb')):
        p.dma(dst[:, :], src[:, :], writes=[tok])
    p.dma(onw[:, :], onw_d.rearrange("(p o) -> p o", o=1), writes=['onw'])
    p.v('pool', 'memset', vnew[:, :, :], 0.0, writes=['vnew'])

    def P3(t):
        return t[:, :].rearrange("p (j e) -> p j e", e=128)

    def bc(m, n):
        return m[:, :].rearrange("p (o e) -> p o e", o=1).to_broadcast([128, n, 128])

    for h in range(NH):
        rows = slice(h * 128, (h + 1) * 128)
        p.dma(tz[:, :], zT[rows, :], writes=['tz'], q='act')
        p.dma(cw[:, :, :], cw_d[h].rearrange("i c w -> c i w"), writes=['cw'])
        p.dma(Gb[:, :], aT[h].partition_broadcast(128), writes=['Gb'])
        p.dma(Bb[:, :], bT[h].partition_broadcast(128), writes=['Bb'])
        p.dma(sc[:, 0:1], Alog[h:h + 1].partition_broadcast(128), writes=['sc'])
        p.dma(sc[:, 1:2], dtb[h:h + 1].partition_broadcast(128), writes=['sc'])
        for i in range(3):
            x_, y_ = tx[i], ty[i]
            xt_, yt_ = 'tx', 'ty%d' % i
            eng = 'dve'
            p.dma(x_[:, :], (qT, kT, vT)[i][rows, :], writes=['tx'], q=('sp' if i != 1 else 'act'))
            p.v(eng, 'tensor_scalar', y_[:, :], x_[:, :], cw[:, i, 3:4], None, ALU.mult, reads=[xt_, 'cw'], writes=[yt_] + AL[i])
            for sft in (1, 2, 3):
                p.v(eng, 'scalar_tensor_tensor', y_[:, sft:T], x_[:, 0:T - sft], cw[:, i, 3 - sft:4 - sft], y_[:, sft:T],
                    ALU.mult, ALU.add, reads=[xt_, 'cw', yt_], writes=[yt_])
            p.act(y_[:, :], y_[:, :], AF.Silu, reads=[yt_], writes=[yt_])
        p.act(tz[:, :], tz[:, :], AF.Silu, reads=['tz'], writes=['tz'])
        if stop <= 1:
            continue
        for i in range(2):
            y_, yt_ = ty[i], 'ty%d' % i
            p.act(tt[:, :], y_[:, :], AF.Square, reads=[yt_], writes=['tt'])
            for q0 in range(0, T, 512):
                qn = min(512, T - q0)
                bt, btk = nb()
                p.mm(bt[:, 0:qn], ones[:, :], tt[:, q0:q0 + qn], True, True, reads=['ones', 'tt'], writes=[btk])
                p.v('dve', 'tensor_scalar', rs[:, 0:qn], bt[:, 0:qn], EPS, None, ALU.add, reads=[btk], writes=['rs'])
                p.act(rs[:, 0:qn], rs[:, 0:qn], AF.Sqrt, reads=['rs'], writes=['rs'])
                p.v('dve', 'reciprocal', rs[:, 0:qn], rs[:, 0:qn], reads=['rs'], writes=['rs'])
                if i == 0:
                    p.v('dve', 'scalar_tensor_tensor', y_[:, q0:q0 + qn], y_[:, q0:q0 + qn], float(128 ** -0.5),
                        rs[:, 0:qn], ALU.mult, ALU.mult, reads=[yt_, 'rs'], writes=[yt_])
                else:
                    p.v('dve', 'tensor_tensor', y_[:, q0:q0 + qn], y_[:, q0:q0 + qn], rs[:, 0:qn], ALU.mult,
                        reads=[yt_, 'rs'], writes=[yt_])
        if stop <= 2:
            continue
        p.act(Bb[:, :], Bb[:, :], AF.Sigmoid, reads=['Bb'], writes=['Bb'])
        p.act(sc[:, 2:3], sc[:, 0:1], AF.Exp, reads=['sc'], writes=['sc2'])
        p.v('dve', 'tensor_scalar', sc[:, 3:4], sc[:, 2:3], -1.0, None, ALU.mult, reads=['sc2'], writes=['sc3'])
        p.v('dve', 'tensor_scalar', Gb[:, :], Gb[:, :], sc[:, 1:2], None, ALU.add, reads=['Gb', 'sc'], writes=['Gb'])
        p.act(Gb[:, :], Gb[:, :], AF.Exp, reads=['Gb'], writes=['Gb'])
        p.v('dve', 'tensor_scalar', Gb[:, :], Gb[:, :], 1.0, None, ALU.add, reads=['Gb'], writes=['Gb'])
        p.act(Gb[:, :], Gb[:, :], AF.Ln, reads=['Gb'], writes=['Gb'])
        p.v('dve', 'tensor_scalar', Gb[:, :], Gb[:, :], sc[:, 3:4], None, ALU.mult, reads=['Gb', 'sc3'], writes=['Gb'])
        res, rtok = cumsum_chunks(p, Gb, tt2, 'Gb', 'tt2', NC)
        assert res is Gb
        g3 = Gb[:, :].rearrange("p (c s) -> p c s", s=C)
        p.act(Eb[:, :], Gb[:, :], AF.Exp, reads=['Gb'], writes=['Eb'])
        p.v('dve', 'tensor_tensor', tt[:, :], Eb[:, :], Bb[:, :], ALU.mult, reads=['Eb', 'Bb'], writes=['tt'])
        p.v('dve', 'tensor_tensor', tt2[:, :].rearrange("p (c s) -> p c s", s=C), g3,
            g3[:, :, C - 1:C].to_broadcast([128, NC, C]), ALU.subtract, reads=['Gb'], writes=['tt2'])
        p.act(tt2[:, :], tt2[:, :], AF.Exp, reads=['tt2'], writes=['tt2'], scale=-1.0)
        if stop <= 3:
            continue
        bt, btk = nb()
        for qi, (rsrc, rt) in enumerate(((Bb, 'Bb'), (Gb, 'Gb'), (tt, 'tt'), (tt2, 'tt2'))):
            for j in range(NP):
                p.mm(bt[:, qi * NP + j:qi * NP + j + 1], rsrc[0:1, j * 128:(j + 1) * 128], ones[0:1, 0:1], True, True,
                     reads=[rt, 'ones'], writes=[btk])
        p.v('dve', 'tensor_copy', cols[:, :, :], bt[:, 0:4 * NP].rearrange("p (q j) -> p q j", j=NP), reads=[btk], writes=['cols'])
        for hh in range(2):
            p.v('dve', 'tensor_scalar', colsH[:, hh, :], cols[:, 3, :], halfm[:, hh:hh + 1], None, ALU.mult,
                reads=['cols', 'halfm'], writes=['colsH'])
        if stop <= 4:
            continue
        p.v('pool', 'tensor_copy', qb[:, :], ty[0][:, :], reads=['ty0'], writes=['qb'])
        p.v('pool', 'tensor_copy', kb[:, :], ty[1][:, :], reads=['ty1'], writes=['kb'])
        p.v('dve', 'tensor_tensor', qdb[:, :], ty[0][:, :], Eb[:, :], ALU.mult, reads=['ty0', 'Eb'], writes=['qdb'])
        for j0 in range(0, NP, 4):
            jn = min(4, NP - j0)
            bk, bkt = nb()
            bv, bvt = nb()
            for j in range(jn):
                cs = slice((j0 + j) * 128, (j0 + j + 1) * 128)
                p.transpose(bk[:, j * 128:(j + 1) * 128], ty[1][:, cs], ident[:, :], reads=['ty1', 'ident'], writes=[bkt])
                p.transpose(bv[:, j * 128:(j + 1) * 128], ty[2][:, cs], ident[:, :], reads=['ty2', 'ident'], writes=[bvt])
            for j in range(jn):
                jj = j0 + j
                ps = slice(j * 128, (j + 1) * 128)
                if 'y' in SKIP:
                    continue
                p.v('dve', 'tensor_scalar', Y[:, jj, 0:128], bv[:, ps], cols[:, 0, jj:jj + 1], None, ALU.mult,
                    reads=[bvt, 'cols'], writes=[('Y', jj // 4)])
                p.v('dve', 'tensor_scalar', Y[:, jj, 128:256], bk[:, ps], cols[:, 2, jj:jj + 1], None, ALU.mult,
                    reads=[bkt, 'cols'], writes=[('Y', jj // 4)])
                for hh in range(2):
                    if 'k' in SKIP:
                        continue
                    p.v('dve', 'tensor_scalar', kdh[hh][:, jj, :], bk[:, ps], colsH[:, hh, jj:jj + 1], None, ALU.mult,
                        reads=[bkt, 'colsH'], writes=['kdh'])
        if stop <= 5:
            continue
        for ps_ in range(2):
            for j in range(NP):
                cs = slice(j * 128, (j + 1) * 128)
                p.v('dve' if j % 2 == 0 else 'pool', 'tensor_scalar', D1[:, j, :], Gb[:, cs], cols[:, 1, j:j + 1], 0.0, ALU.subtract,
                    ALU.min if ps_ == 0 else ALU.max, reads=['Gb', 'cols'], writes=['D1', 'ty0'])
            p.act(D1[:, :, :], D1[:, :, :], AF.Exp, reads=['D1'], writes=['D1'], scale=(1.0 if ps_ == 0 else -1.0))
            for j0 in range(0, NP, 4):
                jn = min(4, NP - j0)
                g = j0 // 4
                js = slice(j0, j0 + jn)
                b1, b1t = nb()
                for j in range(jn):
                    cs = slice((j0 + j) * 128, (j0 + j + 1) * 128)
                    p.mm(b1[:, j * 128:(j + 1) * 128], kb[:, cs], kb[:, cs], True, True, reads=['kb'], writes=[b1t])
                if ps_ == 0:
                    b2, b2t = nb()
                    for j in range(jn):
                        cs = slice((j0 + j) * 128, (j0 + j + 1) * 128)
                        p.mm(b2[:, j * 128:(j + 1) * 128], kb[:, cs], qb[:, cs], True, True, reads=['kb', 'qb'], writes=[b2t])
                    p.v('dve', 'tensor_tensor', Qm[:, js, :], P3(b2)[:, 0:jn, :], D1[:, js, :], ALU.mult, reads=[b2t, 'D1'], writes=[('Q', g), 'ty2'])
                    p.v('pool', 'tensor_tensor', qkT[:, js, :], Qm[:, js, :], bc(mU, jn), ALU.mult, reads=[('Q', g), 'mU'], writes=['qkT'])
                    p.v('dve', 'tensor_tensor', Qm[:, js, :], P3(b1)[:, 0:jn, :], D1[:, js, :], ALU.mult, reads=[b1t, 'D1'], writes=[('Q', g)])
                    p.v('dve', 'tensor_tensor', Qm[:, js, :], Qm[:, js, :], P3(Bb)[:, js, :], ALU.mult, reads=[('Q', g), 'Bb'], writes=[('Q', g)])
                    p.v('pool', 'tensor_tensor', Qm[:, js, :], Qm[:, js, :], bc(mUs, jn), ALU.mult, reads=[('Q', g), 'mUs'], writes=[('Q', g)])
                else:
                    p.v('dve', 'tensor_tensor', Pm[:, js, :], P3(b1)[:, 0:jn, :], D1[:, js, :], ALU.mult, reads=[b1t, 'D1'], writes=[('P', g), 'ty1'])
                    p.v('pool', 'tensor_tensor', Pm[:, js, :], Pm[:, js, :], bc(mLs, jn), ALU.mult, reads=[('P', g), 'mLs'], writes=[('P', g)])
                    for j in range(jn):
                        p.v('pool', 'tensor_scalar', Pm[:, j0 + j, :], Pm[:, j0 + j, :], cols[:, 0, j0 + j:j0 + j + 1], None, ALU.mult,
                            reads=[('P', g), 'cols'], writes=[('P', g)])
        if stop <= 6:
            continue
        for j0 in range(0, NP, 4):
            jn = min(4, NP - j0)
            g = j0 // 4
            js = slice(j0, j0 + jn)
            for lvl in range(6):
                ya, yat = nb()
                yb, ybt = nb()
                for j in range(jn):
                    bt, btk = (ya, yat) if j < 2 else (yb, ybt)
                    p.mm(bt[:, (j % 2) * 256:(j % 2) * 256 + 256], Qm[:, j0 + j, :], Y[:, j0 + j, :], True, True,
                         reads=[('Q', g), ('Y', g)], writes=[btk])
                op = ALU.subtract if lvl == 0 else ALU.add
                for half, (bt, btk) in enumerate(((ya, yat), (yb, ybt))):
                    n2 = min(2, jn - half * 2)
                    if n2 <= 0:
                        continue
                    ys = slice(j0 + half * 2, j0 + half * 2 + n2)
                    p.v('dve', 'tensor_tensor', Y[:, ys, :], Y[:, ys, :],
                        bt[:, 0:n2 * 256].rearrange("p (j e) -> p j e", e=256), op,
                        reads=[('Y', g), btk], writes=[('Y', g)])
                if lvl == 5:
                    break
                b1, b1t = nb()
                b2, b2t = nb()
                for j in range(jn):
                    p.mm(b1[:, j * 128:(j + 1) * 128], Qm[:, j0 + j, :], Pm[:, j0 + j, :], True, True,
                         reads=[('Q', g), ('P', g)], writes=[b1t])
                    p.mm(b2[:, j * 128:(j + 1) * 128], Pm[:, j0 + j, :], Qm[:, j0 + j, :], True, True,
                         reads=[('Q', g), ('P', g)], writes=[b2t])
                p.v('dve', 'tensor_copy', Pm[:, js, :], P3(b1)[:, 0:jn, :], reads=[b1t], writes=[('P', g)])
                p.act(Qm[:, js, :], P3(b2)[:, 0:jn, :], AF.Copy, reads=[b2t], writes=[('Q', g)])
        if stop <= 7:
            continue
        for j0 in range(0, NP, 4):
            g = j0 // 4
            jn = min(4, NP - j0)
            p.v('pool', 'tensor_copy', wtokb[:, j0:j0 + jn, :], Y[:, j0:j0 + jn, 128:256], reads=[('Y', g)], writes=['wtokb'])
        for j0 in range(0, NP, 8):
            bt, btk = bankb[0]
            jn = min(8, NP - j0)
            for j in range(jn):
                p.transpose(bt[:, j * 128:(j + 1) * 128], wtokb[:, j0 + j, :], identb[:, :], reads=['wtokb', 'identb'], writes=[btk])
            p.v('dve', 'tensor_copy', wTb[:, j0 * 128:(j0 + jn) * 128], bt[:, 0:jn * 128], reads=[btk], writes=['wTb'])
        if stop <= 8:
            continue
        p.v('dve', 'memset', S32[0][:, :], 0.0, writes=['S32_0'])
        p.v('pool', 'memset', Sbf[:, 0, :], 0.0, writes=[('Sbf', 0)])
        for c in range(NC):
            j, cc = c // 2, c % 2
            pr = slice(cc * 64, cc * 64 + 64)
            bt, btk = nb()
            p.mm(bt[:, 0:128], wTb[:, j * 128:(j + 1) * 128], Sbf[:, c, :], True, True, reads=['wTb', ('Sbf', c)], writes=[btk])
            p.v('dve', 'tensor_tensor', vnew[pr, c, :], Y[pr, j, 0:128], bt[pr, 0:128], ALU.subtract,
                reads=[('Y', j // 4), btk, 'vnew'], writes=[('vnew', c)])
            if c == NC - 1:
                break
            p.mm(bt[:, 128:256], kdh[cc][:, j, :], vnew[:, c, :], True, True, reads=['kdh', ('vnew', c), 'vnew'], writes=[btk])
            sa, sb = S32[c % 2], S32[(c + 1) % 2]
            p.v('dve', 'scalar_tensor_tensor', sb[:, :], sa[:, :], Eb[:, c * 64 + 63:c * 64 + 64], bt[:, 128:256],
                ALU.mult, ALU.add, reads=['S32_%d' % (c % 2), 'Eb', btk], writes=['S32_%d' % ((c + 1) % 2)])
            p.act(Sbf[:, c + 1, :], sb[:, :], AF.Copy, reads=['S32_%d' % ((c + 1) % 2)], writes=[('Sbf', c + 1)])
        if stop <= 9:
            continue
        for q0 in range(0, T, 512):
            qn = min(512, T - q0)
            bo, botok = nb()
            bs, bstok = nb()
            for cq in range(qn // 64):
                c = q0 // 64 + cq
                j, cc = c // 2, c % 2
                p.mm(bo[:, cq * 64:(cq + 1) * 64], Sbf[:, c, :], qdb[:, c * 64:(c + 1) * 64], True, False,
                     reads=[('Sbf', c), 'qdb'], writes=[botok])
                p.mm(bo[:, cq * 64:(cq + 1) * 64], vnew[:, c, :], qkT[:, j, cc * 64:(cc + 1) * 64], False, True,
                     reads=[('vnew', c), 'vnew', 'qkT'], writes=[botok])
            p.act(o2[:, 0:qn], bo[:, 0:qn], AF.Square, reads=[botok], writes=['o2'])
            p.mm(bs[:, 0:qn], ones[:, :], o2[:, 0:qn], True, True, reads=['ones', 'o2'], writes=[bstok])
            p.v('dve', 'tensor_scalar', rs[:, 0:qn], bs[:, 0:qn], 1.0 / 128, EPS, ALU.mult, ALU.add, reads=[bstok], writes=['rs'])
            p.act(rs[:, 0:qn], rs[:, 0:qn], AF.Sqrt, reads=['rs'], writes=['rs'])
            p.v('dve', 'reciprocal', rs[:, 0:qn], rs[:, 0:qn], reads=['rs'], writes=['rs'])
            p.v('dve', 'tensor_tensor', t1[:, 0:qn], bo[:, 0:qn], rs[:, 0:qn], ALU.mult, reads=[botok, 'rs'], writes=['t1'])
            p.v('dve', 'scalar_tensor_tensor', ob[:, q0:q0 + qn], t1[:, 0:qn], onw[:, 0:1], tz[:, q0:q0 + qn],
                ALU.mult, ALU.mult, reads=['t1', 'onw', 'tz'], writes=['ob'])
        p.dma(oT[rows, :], ob[:, :], reads=['ob'], writes=['oT'])
    nops = p.finish()
    return p.nc, nops


SCALE = 128 ** -0.5


def build_nsa(G, T):
    p = Prog()
    HQ = G * 4
    NP = T // 128
    NR = T // 512
    NCMP = (T - 32) // 16 + 1
    NS = T // 64
    assert NS == 32
    qT = p.dram_in("qT", [HQ * 128, T])
    kv = {n: p.dram_in(n, [G * 128, T]) for n in ("kcT", "vcT", "ksT", "vsT", "kwT", "vwT")}
    gateT = p.dram_in("gateT", [HQ * 3, T])
    pos = {n: p.dram_in("pos_" + n, [32, 128]) for n in "kv"}
    w1 = {n: p.dram_in("w1_" + n, [4096, 256]) for n in "kv"}
    w2 = {n: p.dram_in("w2_" + n, [256, 128]) for n in "kv"}
    cos_d = p.dram_in("cosT", [128, T]); sin_d = p.dram_in("sinS", [128, T])
    cmask_d = p.dram_in("cmask", [NCMP, T]); ovl_d = p.dram_in("overlap", [NCMP, 32])
    selb_d = p.dram_in("selbias", [T, 32]); E_d = p.dram_in("E", [32, T], BF16)
    tri_d = p.dram_in("tri", [128, 128], BF16); atri_d = p.dram_in("atri", [128, 128], BF16)
    ident_d = p.dram_in("ident", [128, 128]); ones_d = p.dram_in("ones", [128, 128])
    onesb_d = p.dram_in("onesb", [128, 128], BF16); identb_d = p.dram_in("identb", [128, 128], BF16)
    gsel_d = p.dram_in("gsel", [24, 24 * 128])
    oT = p.dram_out("oT", [HQ * 128, T], BF16)

    cosT = p.sbuf("cosT", [128, T]); sinS = p.sbuf("sinS", [128, T])
    cmask = p.sbuf("cmask", [128, T]); ovl = p.sbuf("ovl", [128, 32])
    selb = p.sbuf("selb", [128, NP, 32]); E = p.sbuf("E", [32, T], BF16)
    tri = p.sbuf("tri", [128, 128], BF16); atri = p.sbuf("atri", [128, 128], BF16)
    ident = p.sbuf("ident", [128, 128]); ones = p.sbuf("ones", [128, 128])
    onesb = p.sbuf("onesb", [128, 128], BF16); identb = p.sbuf("identb", [128, 128], BF16)
    gsel = p.sbuf("gsel", [24, 24 * 128]); sgate = p.sbuf("sgate", [24, T])
    tr = p.sbuf("tr", [128, T]); ts = p.sbuf("ts", [128, T]); tu = p.sbuf("tu", [128, T])
    kcR = p.sbuf("kcR", [128, T])
    w1s = p.sbuf("w1s", [128, 32, 256], BF16); w2s = p.sbuf("w2s", [128, 2, 128], BF16); posT = p.sbuf("posT", [128, 32])
    blk = p.sbuf("blk", [128, 32, NCMP], BF16)
    hx = p.sbuf("hx", [128, 2, NCMP]); hu = p.sbuf("hu", [128, 2, NCMP]); gh = p.sbuf("gh", [128, 2, NCMP], BF16)
    kcmpT = p.sbuf("kcmpT", [128, NCMP], BF16); vcmp = p.sbuf("vcmp", [128, 128], BF16)
    ksT = p.sbuf("ksT", [128, T], BF16); kwT = p.sbuf("kwT", [128, T], BF16)
    vb = p.sbuf("vb", [128, T], BF16)
    vtok = {n: p.sbuf(n + "tok", [128, NP, 128], BF16) for n in ("vs", "vw")}
    qb = p.sbuf("qb", [128, 4, T], BF16)
    P32 = p.sbuf("P32", [128, 512]); Pc = p.sbuf("Pc", [128, 512], BF16)
    Pt = [p.sbuf("Pt%d" % i, [128, 512], BF16) for i in range(3)]
    impT = p.sbuf("impT", [32, T]); selbT = p.sbuf("selbT", [32, T], BF16)
    rd = p.sbuf("rd", [128, 512]); fac = p.sbuf("fac", [128, 512]); tmp = p.sbuf("tmp", [128, 512])
    oacc = p.sbuf("oacc", [128, 4, 512]); ob = p.sbuf("ob", [128, 512], BF16)
    wk = p.sbuf("wk", [128, 32]); wk2 = p.sbuf("wk2", [128, 32]); m8 = p.sbuf("m8", [128, 16]); sel = p.sbuf("sel", [128, 32])
    psS = [(p.psum("psS%d" % i, [128, 512]), 'psS%d' % i) for i in range(3)]
    po, pot = p.psum("po", [128, 512]), 'po'
    pd, pdt = p.psum("pd", [128, 512]), 'pd'
    pm = [(p.psum("pm%d" % i, [128, 512]), 'pm%d' % i) for i in range(2)]
    pb, pbt = p.psum("pb", [128, 1024], BF16), 'pb'
    cnt = {'s': 0, 'm': 0, 'p': 0}

    def nS():
        cnt['s'] += 1
        return psS[cnt['s'] % 3]

    def nM():
        cnt['m'] += 1
        return pm[cnt['m'] % 2]

    def nP():
        cnt['p'] += 1
        return Pt[cnt['p'] % 3], 'Pt%d' % (cnt['p'] % 3)

    for dst, src, tok in ((cosT, cos_d, 'cosT'), (sinS, sin_d, 'sinS'), (E, E_d, 'E'), (tri, tri_d, 'tri'), (atri, atri_d, 'atri'),
                          (ident, ident_d, 'ident'), (ones, ones_d, 'ones'), (onesb, onesb_d, 'onesb'), (identb, identb_d, 'identb'),
                          (gsel, gsel_d, 'gsel')):
        p.dma(dst[:, :], src[:, :], writes=[tok])
    p.dma(cmask[0:NCMP, :], cmask_d[:, :], writes=['cmask'])
    p.dma(ovl[0:NCMP, :], ovl_d[:, :], writes=['ovl'])
    p.dma(selb[:, :, :], selb_d.rearrange("(j p) n -> p j n", p=128), writes=['selb'])
    p.dma(sgate[:, :], gateT[:, :], writes=['sgate'])
    p.act(sgate[:, :], sgate[:, :], AF.Sigmoid, reads=['sgate'], writes=['sgate'])

    def rope(src_rows, dst, dtok):
        p.dma(tr[:, :], src_rows, writes=['tr'])
        p.dma(ts[0:64, :], tr[64:128, :], reads=['tr'], writes=['ts'], q='act')
        p.dma(ts[64:128, :], tr[0:64, :], reads=['tr'], writes=['ts'], q='act')
        p.v('dve', 'tensor_tensor', tu[:, :], tr[:, :], cosT[:, :], ALU.mult, reads=['tr', 'cosT'], writes=['tu'])
        p.v('pool', 'tensor_tensor', ts[:, :], ts[:, :], sinS[:, :], ALU.mult, reads=['ts', 'sinS'], writes=['ts'])
        p.v('dve', 'tensor_tensor', dst, tu[:, :], ts[:, :], ALU.add, reads=['tu', 'ts'], writes=[dtok])

    def compress(n, src_tile, stok, g):
        for l0 in range(0, 32, 8):
            p.dma(w1s[:, l0:l0 + 8, :], w1[n][l0 * 128:(l0 + 8) * 128, :].rearrange("(l d) h -> d l h", d=128),
                  writes=['w1s'], q='pool')
        p.dma(w2s[:, :, :], w2[n].rearrange("(c p) d -> p c d", p=128), writes=['w2s'], q='pool')
        p.dma(posT[:, :], pos[n].rearrange("l d -> d l"), writes=['posT'], allow_slow_non_contiguous=True)
        a = src_tile[:, :]
        win = bass.AP(a.tensor, a.offset, [list(a.ap[0]), [1, 32], [16, NCMP]])
        p.v('dve', 'tensor_tensor', blk[:, :, :], win,
            posT[:, :].rearrange("p (l o) -> p l o", o=1).to_broadcast([128, 32, NCMP]), ALU.add,
            reads=[stok, 'posT'], writes=['blk'])
        for hc in range(2):
            bt, btk = nM()
            for l in range(32):
                p.mm(bt[:, 0:NCMP], w1s[:, l, hc * 128:(hc + 1) * 128], blk[:, l, :], l == 0, l == 31,
                     reads=['w1s', 'blk'], writes=[btk])
            p.v('dve', 'tensor_copy', hx[:, hc, :], bt[:, 0:NCMP], reads=[btk], writes=['hx'])
        p.v('dve', 'tensor_tensor', hu[:, :, :], hx[:, :, :], hx[:, :, :], ALU.mult, reads=['hx'], writes=['hu'])
        p.v('dve', 'tensor_scalar', hu[:, :, :], hu[:, :, :], 0.044715, 1.0, ALU.mult, ALU.add, reads=['hu'], writes=['hu'])
        p.v('dve', 'tensor_tensor', hu[:, :, :], hu[:, :, :], hx[:, :, :], ALU.mult, reads=['hu', 'hx'], writes=['hu'])
        p.act(hu[:, :, :], hu[:, :, :], AF.Sigmoid, reads=['hu'], writes=['hu'], scale=1.5957691216057308)
        p.v('dve', 'tensor_tensor', gh[:, :, :], hu[:, :, :], hx[:, :, :], ALU.mult, reads=['hu', 'hx'], writes=['gh'])
        bt, btk = nM()
        if n == 'k':
            for hc in range(2):
                p.mm(bt[:, 0:NCMP], w2s[:, hc, :], gh[:, hc, :], hc == 0, hc == 1, reads=['w2s', 'gh'], writes=[btk])
            p.v('dve', 'tensor_copy', kcmpT[:, :], bt[:, 0:NCMP], reads=[btk], writes=['kcmpT'])
        else:
            for hc in range(2):
                p.mm(bt[0:NCMP, 0:128], gh[:, hc, :], w2s[:, hc, :], hc == 0, hc == 1, reads=['w2s', 'gh'], writes=[btk])
            p.v('dve', 'tensor_copy', vcmp[0:NCMP, :], bt[0:NCMP, 0:128], reads=[btk], writes=['vcmp'])

    def finish_branch(h, hl, br, r, first):
        cs = slice(r * 512, (r + 1) * 512)
        p.v('dve', 'tensor_scalar', rd[:, :], pd[:, :], 1e-30, None, ALU.max, reads=[pdt], writes=['rd'])
        p.v('dve', 'reciprocal', rd[:, :], rd[:, :], reads=['rd'], writes=['rd'])
        pg, pgt = nM()
        row = h * 3 + br
        p.mm(pg[:, :], gsel[:, row * 128:(row + 1) * 128], sgate[:, cs], True, True, reads=['gsel', 'sgate'], writes=[pgt])
        p.v('dve', 'tensor_tensor', fac[:, :], rd[:, :], pg[:, :], ALU.mult, reads=['rd', pgt], writes=['fac'])
        if first:
            p.v('dve', 'tensor_tensor', oacc[:, hl, :], po[:, :], fac[:, :], ALU.mult, reads=[pot, 'fac'], writes=[('oacc', hl)])
        else:
            p.v('dve', 'tensor_tensor', tmp[:, :], po[:, :], fac[:, :], ALU.mult, reads=[pot, 'fac'], writes=['tmp'])
            p.v('pool', 'tensor_tensor', oacc[:, hl, :], oacc[:, hl, :], tmp[:, :], ALU.add, reads=[('oacc', hl), 'tmp'], writes=[('oacc', hl)])

    def attn(hl, r, KT, ktok_, vt, vtok_, chunks, slc):
        for ci, (kc, c0, c1, masks) in enumerate(chunks):
            ps, pst = nS()
            ks_ = slice(kc * 128, (kc + 1) * 128)
            qs_ = slice(r * 512 + c0, r * 512 + c1)
            p.mm(ps[:, c0:c1], KT[:, ks_], qb[:, hl, qs_], True, not slc, reads=[ktok_, ('qb', hl)], writes=[pst])
            if slc:
                p.mm(ps[:, c0:c1], E[:, ks_], selbT[:, qs_], False, True, reads=['E', 'selbT'], writes=[pst])
            pt_, ptt = nP()
            p.act(pt_[:, c0:c1], ps[:, c0:c1], AF.Exp, reads=[pst], writes=[ptt], scale=SCALE)
            for (m0, mt, mtok) in masks:
                p.v('pool', 'tensor_tensor', pt_[:, m0:m0 + 128], pt_[:, m0:m0 + 128], mt[:, :], ALU.mult,
                    reads=[ptt, mtok], writes=[ptt])
            first, last = ci == 0, ci == len(chunks) - 1
            p.mm(po[:, c0:c1], vt[:, kc, :], pt_[:, c0:c1], first, last, reads=[vtok_, ptt], writes=[pot])
            p.mm(pd[:, c0:c1], onesb[:, :], pt_[:, c0:c1], first, last, reads=['onesb', ptt], writes=[pdt])

    for g in range(G):
        grow = slice(g * 128, (g + 1) * 128)
        rope(kv["kcT"][grow, :], kcR[:, :], 'kcR')
        compress('k', kcR, 'kcR', g)
        p.dma(tr[:, :], kv["vcT"][grow, :], writes=['tr'])
        compress('v', tr, 'tr', g)
        rope(kv["ksT"][grow, :], ksT[:, :], 'ksT')
        rope(kv["kwT"][grow, :], kwT[:, :], 'kwT')
        for n in ("vs", "vw"):
            p.dma(tr[:, :], kv[n + "T"][grow, :], writes=['tr'])
            p.v('pool', 'tensor_copy', vb[:, :], tr[:, :], reads=['tr'], writes=['vb'])
            for j0 in range(0, NP, 8):
                jn = min(8, NP - j0)
                for j in range(jn):
                    p.transpose(pb[:, j * 128:(j + 1) * 128], vb[:, (j0 + j) * 128:(j0 + j + 1) * 128], identb[:, :],
                                reads=['vb', 'identb'], writes=[pbt])
                p.v('dve', 'tensor_copy', vtok[n][:, j0:j0 + jn, :], pb[:, 0:jn * 128].rearrange("p (j e) -> p j e", e=128),
                    reads=[pbt], writes=[n + 'tok'])
        for hl in range(4):
            h = g * 4 + hl
            rope(qT[h * 128:(h + 1) * 128, :], qb[:, hl, :], ('qb', hl))
        for r in range(NR):
            cs = slice(r * 512, (r + 1) * 512)
            for hl in range(4):
                h = g * 4 + hl
                ps, pst = nS()
                p.mm(ps[0:NCMP, :], kcmpT[:, :], qb[:, hl, cs], True, True, reads=['kcmpT', ('qb', hl)], writes=[pst])
                p.act(P32[0:NCMP, :], ps[0:NCMP, :], AF.Exp, reads=[pst], writes=['P32'], scale=SCALE)
                p.v('dve', 'tensor_tensor', P32[0:NCMP, :], P32[0:NCMP, :], cmask[0:NCMP, cs], ALU.mult,
                    reads=['P32', 'cmask'], writes=['P32'])
                p.v('pool', 'tensor_copy', Pc[0:NCMP, :], P32[0:NCMP, :], reads=['P32'], writes=['Pc'])
                p.mm(po[:, :], vcmp[0:NCMP, :], Pc[0:NCMP, :], True, True, reads=['vcmp', 'Pc'], writes=[pot])
                p.mm(pd[:, :], ones[0:NCMP, :], P32[0:NCMP, :], True, True, reads=['ones', 'P32'], writes=[pdt])
                pi, pit = nM()
                p.mm(pi[0:32, :], ovl[0:NCMP, :], P32[0:NCMP, :], True, True, reads=['ovl', 'P32'], writes=[pit])
                finish_branch(h, hl, 0, r, True)
                if hl == 0:
                    p.v('dve', 'tensor_tensor', impT[:, cs], pi[0:32, :], rd[0:32, :], ALU.mult, reads=[pit, 'rd'], writes=['impT'])
                else:
                    p.v('dve', 'tensor_tensor', tmp[0:32, :], pi[0:32, :], rd[0:32, :], ALU.mult, reads=[pit, 'rd'], writes=['tmp'])
                    p.v('dve', 'tensor_tensor', impT[:, cs], impT[:, cs], tmp[0:32, :], ALU.add, reads=['impT', 'tmp'], writes=['impT'])
            for i in range(4 * r, 4 * r + 4):
                ts_ = slice(i * 128, (i + 1) * 128)
                pt1, pt1t = nM()
                p.transpose(pt1[:, 0:32], impT[:, ts_], ident[0:32, 0:32], reads=['impT', 'ident'], writes=[pt1t])
                p.v('dve', 'tensor_tensor', wk[:, :], pt1[:, 0:32], selb[:, i, :], ALU.add, reads=[pt1t, 'selb'], writes=['wk'])
                p.v('dve', 'max', m8[:, 0:8], wk[:, :], reads=['wk'], writes=['m8'])
                p.v('dve', 'match_replace', wk2[:, :], m8[:, 0:8], wk[:, :], -1e9, reads=['wk', 'm8'], writes=['wk2'])
                p.v('dve', 'max', m8[:, 8:16], wk2[:, :], reads=['wk2'], writes=['m8'])
                p.v('dve', 'tensor_scalar', sel[:, :], wk[:, :], m8[:, 15:16], None, ALU.is_ge, reads=['wk', 'm8'], writes=['sel'])
                p.v('dve', 'tensor_scalar', sel[:, :], sel[:, :], -1.0, 30000.0, ALU.add, ALU.mult, reads=['sel'], writes=['sel'])
                pt2, pt2t = nM()
                p.transpose(pt2[0:32, 0:128], sel[:, :], ident[:, :], reads=['sel', 'ident'], writes=[pt2t])
                p.v('dve', 'tensor_copy', selbT[:, ts_], pt2[0:32, 0:128], reads=[pt2t], writes=['selbT'])
            for hl in range(4):
                h = g * 4 + hl
                chunks = []
                for kc in range(0, 4 * r + 4):
                    if kc < 4 * r:
                        chunks.append((kc, 0, 512, []))
                    else:
                        c0 = (kc - 4 * r) * 128
                        chunks.append((kc, c0, 512, [(c0, tri, 'tri')]))
                attn(hl, r, ksT, 'ksT', vtok["vs"], 'vstok', chunks, True)
                finish_branch(h, hl, 1, r, False)
                order = ([4 * r - 1] if r > 0 else []) + [kc for kc in range(max(0, 4 * r - 4), 4 * r + 4) if not (r > 0 and kc == 4 * r - 1)]
                chunks = []
                for kc in order:
                    i0 = max(kc, 4 * r)
                    i1 = min(kc + 4, 4 * r + 3)
                    masks = []
                    if i0 <= kc <= i1:
                        masks.append(((kc - 4 * r) * 128, tri, 'tri'))
                    if i0 <= kc + 4 <= i1:
                        masks.append(((kc + 4 - 4 * r) * 128, atri, 'atri'))
                    chunks.append((kc, (i0 - 4 * r) * 128, (i1 - 4 * r + 1) * 128, masks))
                attn(hl, r, kwT, 'kwT', vtok["vw"], 'vwtok', chunks, False)
                finish_branch(h, hl, 2, r, False)
                p.v('dve', 'tensor_copy', ob[:, :], oacc[:, hl, :], reads=[('oacc', hl)], writes=['ob'])
                p.dma(oT[h * 128:(h + 1) * 128, cs], ob[:, :], reads=['ob'], writes=['oT'], q='act')
    nops = p.finish()
    return p.nc, nops

import numpy as np, ml_dtypes
def nsa_consts(T):
    f32 = np.float32
    bf = ml_dtypes.bfloat16
    NCMP = (T - 32) // 16 + 1
    NS = T // 64
    inv = (np.float32(10000.0) ** (-(np.arange(0, 128, 2, dtype=f32) / np.float32(128)))).astype(f32)
    ang = (np.arange(T, dtype=f32)[:, None] * inv[None, :]).astype(f32)
    c, s = np.cos(ang).astype(f32), np.sin(ang).astype(f32)
    cosT = np.concatenate([c, c], 1).T.copy()
    sinS = np.concatenate([-s, s], 1).T.copy()
    t = np.arange(T); n = np.arange(NCMP); j = np.arange(NS)
    cmask = ((16 * n[:, None] + 31) <= t[None, :]).astype(f32)
    ovl = (np.clip(np.minimum(16 * n[:, None] + 32, 64 * j[None, :] + 64) - np.maximum(16 * n[:, None], 64 * j[None, :]), 0, None) / 32.0).astype(f32)
    cur = t // 64
    forced = (j[None, :] == 0) | (j[None, :] == cur[:, None]) | (j[None, :] == cur[:, None] - 1)
    invalid = (64 * j[None, :]) > t[:, None]
    selb = np.where(invalid, -1000.0, np.where(forced, 1000.0, 0.0)).astype(f32)
    E = (t[None, :] // 64 == j[:, None]).astype(bf)
    a = np.arange(128)
    tri = (a[:, None] <= a[None, :]).astype(bf); atri = (a[:, None] > a[None, :]).astype(bf)
    gsel = np.zeros((24, 24 * 128), f32)
    for r in range(24):
        gsel[r, r * 128:(r + 1) * 128] = 1
    return dict(cosT=cosT, sinS=sinS, cmask=cmask, overlap=ovl, selbias=selb, E=E, tri=tri, atri=atri,
                ident=np.eye(128, dtype=f32), ones=np.ones((128, 128), f32), onesb=np.ones((128, 128), bf),
                identb=np.eye(128).astype(bf), gsel=gsel)


D_MODEL = 4096
SEQ = 2048
BATCH = 4
DEPTH = 2
N_IN = 13376
FFN = 11008
_PROGS = {}


def _prog(key, fn):
    if key not in _PROGS:
        _PROGS[key] = fn()[0]
    return _PROGS[key]


def _run(nc, in_maps):
    res = run_bass_kernel_spmd(nc, in_maps, core_ids=list(range(8)))
    return res.results


def _gd_consts():
    s = np.arange(128)
    same = (s[:, None] // 64 == s[None, :] // 64)
    bf = ml_dtypes.bfloat16
    return dict(ident=np.eye(128, dtype=np.float32), identb=np.eye(128).astype(bf), ones=np.ones((128, 128), np.float32),
                maskU=(same & (s[:, None] <= s[None, :])).astype(np.float32),
                maskUs=(same & (s[:, None] < s[None, :])).astype(np.float32),
                maskLs=(same & (s[:, None] > s[None, :])).astype(np.float32),
                halfm=np.stack([(s < 64), (s >= 64)], 1).astype(np.float32))


def kernel(x, attn_norm, w_in, w_out, ffn_norm, w_gate, w_up, w_down, final_norm,
           hgrn_lb_logits, hgrn_out_norm, gdn_conv, gdn_A_log, gdn_dt_bias, gdn_out_norm,
           cmp_pos_k, cmp_w1_k, cmp_w2_k, cmp_pos_v, cmp_w1_v, cmp_w2_v):
    f32 = np.float32
    bf = ml_dtypes.bfloat16
    A = lambda a: np.ascontiguousarray(np.asarray(a, dtype=f32))
    T = SEQ
    xs = A(x).reshape(BATCH * SEQ, D_MODEL)
    xsh = [np.ascontiguousarray(xs[c * 1024:(c + 1) * 1024]) for c in range(8)]
    ident = np.eye(128, dtype=f32)
    gdc = _gd_consts()
    nsc = nsa_consts(T)
    hgc = dict(identb=gdc["identb"], maskbd=gdc["maskU"], ones=gdc["ones"], halfm=gdc["halfm"])
    for l in range(DEPTH):
        nc1 = _prog("p1", lambda: build_p1(1024, D_MODEL, N_IN))
        wl = A(w_in[l]); nl = A(attn_norm[l])
        r1 = _run(nc1, [dict(x=xsh[c], w=wl, norm=nl, ident=ident) for c in range(8)])
        proj = [np.concatenate([r1[2 * b]["projT"], r1[2 * b + 1]["projT"]], axis=1) for b in range(BATCH)]
        del r1
        R = lambda b, r0, n: np.ascontiguousarray(proj[b][r0:r0 + n])
        nch = _prog(("hg", l), lambda: build_hgrn(4, T, l))
        ims = []
        for c in range(8):
            b, hh = c // 2, c % 2
            o = hh * 512
            ims.append(dict(qT=R(b, o, 512), fT=R(b, 1024 + o, 512), iT=R(b, 2048 + o, 512), gT=R(b, 3072 + o, 512),
                            lbl=np.ascontiguousarray(A(hgrn_lb_logits)[:, o:o + 512]), onw=A(hgrn_out_norm[l]), **hgc))
        ra = _run(nch, ims)
        ncg = _prog("gd", lambda: build_gdn(4, T))
        conv = A(gdn_conv[l])
        ims = []
        for c in range(8):
            b, hh = c // 2, c % 2
            o = hh * 512
            cw = np.stack([conv[:, i * 1024 + o:i * 1024 + o + 512].reshape(4, 4, 128).transpose(1, 2, 0) for i in range(3)], 1)
            ims.append(dict(qT=R(b, 4096 + o, 512), kT=R(b, 5120 + o, 512), vT=R(b, 6144 + o, 512), zT=R(b, 7168 + o, 512),
                            aT=R(b, 8192 + hh * 4, 4), bT=R(b, 8200 + hh * 4, 4), cw=np.ascontiguousarray(cw),
                            Alog=np.ascontiguousarray(A(gdn_A_log[l])[hh * 4:hh * 4 + 4]),
                            dtb=np.ascontiguousarray(A(gdn_dt_bias[l])[hh * 4:hh * 4 + 4]), onw=A(gdn_out_norm[l]), **gdc))
        rb = _run(ncg, ims)
        ncn = _prog("ns", lambda: build_nsa(2, T))
        ims = []
        for c in range(8):
            b, hh = c // 2, c % 2
            im = dict(qT=R(b, 8208 + hh * 1024, 1024), gateT=R(b, 13328 + hh * 24, 24), **nsc)
            for i, n in enumerate(("kcT", "vcT", "ksT", "vsT", "kwT", "vwT")):
                im[n] = R(b, 10256 + i * 512 + hh * 256, 256)
            im.update(pos_k=A(cmp_pos_k[l]), w1_k=A(cmp_w1_k[l]), w2_k=A(cmp_w2_k[l]),
                      pos_v=A(cmp_pos_v[l]), w1_v=A(cmp_w1_v[l]), w2_v=A(cmp_w2_v[l]))
            ims.append(im)
        rc = _run(ncn, ims)
        del proj
        mix = []
        for b in range(BATCH):
            m = np.empty((4096, T), dtype=bf)
            for hh in range(2):
                c = 2 * b + hh
                m[hh * 512:(hh + 1) * 512] = ra[c]["oT"]
                m[1024 + hh * 512:1024 + (hh + 1) * 512] = rb[c]["oT"]
                m[2048 + hh * 1024:2048 + (hh + 1) * 1024] = rc[c]["oT"]
            mix.append(m)
        last = (l == DEPTH - 1)
        nc5 = _prog(("p5", last), lambda: build_p5(1024, D_MODEL, FFN, last))
        wo, wg, wu, wd = A(w_out[l]), A(w_gate[l]), A(w_up[l]), A(w_down[l])
        fn_, fin = A(ffn_norm[l]), A(final_norm)
        ims = []
        for c in range(8):
            b, half = c // 2, c % 2
            ims.append(dict(x=xsh[c], mixT=np.ascontiguousarray(mix[b][:, half * 1024:(half + 1) * 1024]),
                            w_out=wo, w_gate=wg, w_up=wu, w_down=wd, ffn_norm=fn_, final_norm=fin, ident=ident))
        r5 = _run(nc5, ims)
        xsh = [np.ascontiguousarray(r5[c]["y"]) for c in range(8)]
    out = np.concatenate(xsh, axis=0).reshape(BATCH, SEQ, D_MODEL).astype(np.float32)
    return out
```
